# Optimizing a Trainium2 kernel written in Bass

```python
import jax, jax.numpy as jnp
from jax import lax
import numpy as np

D_MODEL = 2048
BATCH = 4
SEQ = 2048
DEPTH = 2
DEC_BATCH = 128
DEC_SEQ = 8
PAST_LEN = 16384
PAGE_SIZE = 128

N_META = 16
POOL_WIDTH = D_MODEL
POOL_WINDOWS = (2, 4, 8, 16)
N_POOL_GROUPS = len(POOL_WINDOWS)
POOL_GROUP = POOL_WIDTH // N_POOL_GROUPS
POOL_HIST = max(POOL_WINDOWS) - 1
RET_HEADS = 8
RET_QK_DIM = D_MODEL // RET_HEADS
RET_V_DIM = D_MODEL // RET_HEADS
RET_QK_WIDTH = RET_HEADS * RET_QK_DIM
RET_V_WIDTH = RET_HEADS * RET_V_DIM
RET_CHUNK = 128
ROPE_BASE = 10000.0
EPS = 1e-6
IN_SIZES = (POOL_WIDTH, POOL_WIDTH, RET_QK_WIDTH, RET_QK_WIDTH, RET_V_WIDTH, RET_V_WIDTH, D_MODEL, D_MODEL)
IN_OFFSETS = tuple(int(o) for o in np.cumsum(IN_SIZES)[:-1])
N_IN = sum(IN_SIZES)

kernel_name = 'hybrid_pool_retention_step'

F32 = jnp.float32


def rms_norm(x, g):
    xf = x.astype(F32)
    y = xf * lax.rsqrt(jnp.mean(xf * xf, axis=-1, keepdims=True) + EPS)
    return (y * g.astype(F32)).astype(x.dtype)


def head_norm(o):
    mu = jnp.mean(o, axis=-1, keepdims=True)
    oc = o - mu
    return oc * lax.rsqrt(jnp.mean(oc * oc, axis=-1, keepdims=True) + EPS)


def log_decay():
    return jnp.log1p(-jnp.exp2(-5.0 - jnp.arange(RET_HEADS, dtype=F32)))


def rope(t, pos):
    half = t.shape[-1] // 2
    inv_freq = ROPE_BASE ** (-jnp.arange(half, dtype=F32) / half)
    ang = pos[:, None] * inv_freq[None, :]
    cos = jnp.cos(ang)[None, :, None, :]
    sin = jnp.sin(ang)[None, :, None, :]
    t1, t2 = t[..., :half], t[..., half:]
    return jnp.concatenate([t1 * cos - t2 * sin, t2 * cos + t1 * sin], axis=-1)


def retention_chunk(S, q, k, v):
    n = q.shape[1]
    lg = log_decay()
    idx = jnp.arange(n, dtype=F32)
    diff = idx[:, None] - idx[None, :]
    causal = diff >= 0
    decay = jnp.where(causal[None], jnp.exp(jnp.where(causal, diff, 0.0)[None] * lg[:, None, None]), 0.0)
    scores = jnp.einsum('bihd,bjhd->bhij', q, k) * decay[None]
    intra = jnp.einsum('bhij,bjhe->bihe', scores, v)
    q_decay = jnp.exp((idx[:, None] + 1.0) * lg[None, :])
    inter = jnp.einsum('bihd,bhde->bihe', q, S) * q_decay[None, :, :, None]
    k_decay = jnp.exp((n - 1.0 - idx)[:, None] * lg[None, :])
    S_new = S * jnp.exp(n * lg)[None, :, None, None] + jnp.einsum('bjhd,bjhe->bhde', k * k_decay[None, :, :, None], v)
    return S_new, intra + inter


def retention_prompt(q, k, v):
    b, L = q.shape[:2]
    S0 = jnp.zeros((b, RET_HEADS, RET_QK_DIM, RET_V_DIM), F32)
    S1, o_meta = retention_chunk(S0, q[:, :N_META], k[:, :N_META], v[:, :N_META])
    n_chunks = (L - N_META) // RET_CHUNK

    def to_chunks(t):
        return t[:, N_META:].reshape(b, n_chunks, RET_CHUNK, *t.shape[2:]).swapaxes(0, 1)

    def step(S, qkv):
        return retention_chunk(S, *qkv)

    S_fin, o = lax.scan(step, S1, (to_chunks(q), to_chunks(k), to_chunks(v)))
    o = o.swapaxes(0, 1).reshape(b, n_chunks * RET_CHUNK, RET_HEADS, RET_V_DIM)
    return S_fin, jnp.concatenate([o_meta, o], axis=1)


def pool_mix(u, hist, pos0, pool_w, pool_scale):
    b, n = u.shape[:2]
    uf = u.astype(F32)
    ext = jnp.concatenate([hist.astype(F32), uf], axis=1)
    cs = jnp.concatenate([jnp.zeros_like(ext[:, :1]), jnp.cumsum(ext, axis=1)], axis=1)
    avail = pos0 + jnp.arange(n, dtype=F32) + 1.0
    groups = []
    for g, w in enumerate(POOL_WINDOWS):
        sl = slice(g * POOL_GROUP, (g + 1) * POOL_GROUP)
        wsum = cs[:, POOL_HIST + 1:POOL_HIST + 1 + n, sl] - cs[:, POOL_HIST + 1 - w:POOL_HIST + 1 - w + n, sl]
        cnt = jnp.minimum(float(w), avail)
        groups.append(wsum / cnt[None, :, None])
    pooled = jnp.concatenate(groups, axis=-1) - uf
    mixed = jnp.einsum('bngc,gcd->bngd', pooled.reshape(b, n, N_POOL_GROUPS, POOL_GROUP), pool_w.astype(F32))
    out = mixed.reshape(b, n, POOL_WIDTH) * pool_scale.astype(F32)
    return out.astype(u.dtype), ext[:, -POOL_HIST:].astype(hist.dtype)


def mixer_layer(x, pos, pos0, pool_hist, ret_state, is_prompt, norm_g, w_in, pool_w, pool_scale,
                ret_gn, proj_pool, proj_ret, w_out):
    b, n = x.shape[:2]
    h = rms_norm(x, norm_g)
    z = h @ w_in
    u, pg, q, k, v, rg, gp, gr = jnp.split(z, IN_OFFSETS, axis=-1)
    pool_out, new_hist = pool_mix(u, pool_hist, pos0, pool_w, pool_scale)
    pool_branch = (pool_out * jax.nn.silu(pg)) @ proj_pool
    qh = rope(q.reshape(b, n, RET_HEADS, RET_QK_DIM).astype(F32), pos)
    kh = rope(k.reshape(b, n, RET_HEADS, RET_QK_DIM).astype(F32), pos) * (RET_QK_DIM ** -0.5)
    vh = v.reshape(b, n, RET_HEADS, RET_V_DIM).astype(F32)
    if is_prompt:
        S_new, o = retention_prompt(qh, kh, vh)
    else:
        S_new, o = retention_chunk(ret_state.astype(F32), qh, kh, vh)
    o = head_norm(o).reshape(b, n, RET_V_WIDTH) * ret_gn.astype(F32)
    ret_branch = (o.astype(x.dtype) * jax.nn.silu(rg)) @ proj_ret
    merged = jax.nn.sigmoid(gp) * pool_branch + jax.nn.sigmoid(gr) * ret_branch
    return x + merged @ w_out, new_hist, S_new


def setup_inputs(seed: int = 0) -> dict:
    key = jax.random.key(seed)
    ks = jax.random.split(key, 14)

    def nrm(k, shape, scale):
        return jax.random.normal(k, shape, F32) * scale

    gam = 1.0 - 2.0 ** (-5.0 - np.arange(RET_HEADS))
    ret_scale = jnp.asarray((RET_QK_DIM ** -0.5) / np.sqrt(1.0 - gam ** 2), F32)
    state_ret = nrm(ks[3], (DEPTH, DEC_BATCH, RET_HEADS, RET_QK_DIM, RET_V_DIM), 1.0) * ret_scale[None, None, :, None, None]
    return {
        'x_prompt': nrm(ks[0], (BATCH, SEQ, D_MODEL), 1.0),
        'x_sample': nrm(ks[1], (DEC_BATCH, DEC_SEQ, D_MODEL), 1.0),
        'state_pool': nrm(ks[2], (DEPTH, DEC_BATCH, POOL_HIST, POOL_WIDTH), 1.0),
        'state_ret': state_ret,
        'meta_tokens': nrm(ks[4], (N_META, D_MODEL), 1.0),
        'norm_gain': 1.0 + nrm(ks[5], (DEPTH, D_MODEL), 0.02),
        'w_in': nrm(ks[6], (DEPTH, D_MODEL, N_IN), D_MODEL ** -0.5),
        'pool_w': nrm(ks[7], (DEPTH, N_POOL_GROUPS, POOL_GROUP, POOL_GROUP), POOL_GROUP ** -0.5),
        'pool_scale': 1.0 + nrm(ks[8], (DEPTH, POOL_WIDTH), 0.02),
        'ret_gn_gain': 1.0 + nrm(ks[9], (DEPTH, RET_V_WIDTH), 0.02),
        'proj_pool': nrm(ks[10], (DEPTH, POOL_WIDTH, D_MODEL), POOL_WIDTH ** -0.5),
        'proj_ret': nrm(ks[11], (DEPTH, RET_V_WIDTH, D_MODEL), RET_V_WIDTH ** -0.5),
        'w_out': nrm(ks[12], (DEPTH, D_MODEL, D_MODEL), D_MODEL ** -0.5),
        'final_norm': 1.0 + nrm(ks[13], (D_MODEL,), 0.02),
    }


def reference(x_prompt, x_sample, state_pool, state_ret, meta_tokens, norm_gain, w_in, pool_w,
              pool_scale, ret_gn_gain, proj_pool, proj_ret, w_out, final_norm):
    b = x_prompt.shape[0]
    meta = jnp.broadcast_to(meta_tokens.astype(x_prompt.dtype)[None], (b, N_META, D_MODEL))
    xp = jnp.concatenate([meta, x_prompt], axis=1)
    pos_p = jnp.arange(xp.shape[1], dtype=F32)
    zero_hist = jnp.zeros((b, POOL_HIST, POOL_WIDTH), x_prompt.dtype)
    xs = x_sample
    pos_s = PAST_LEN + jnp.arange(x_sample.shape[1], dtype=F32)
    pool_p, ret_p, pool_s, ret_s = [], [], [], []
    for l in range(DEPTH):
        xp, hp, Sp = mixer_layer(xp, pos_p, 0, zero_hist, None, True, norm_gain[l], w_in[l], pool_w[l],
                                 pool_scale[l], ret_gn_gain[l], proj_pool[l], proj_ret[l], w_out[l])
        xs, hs, Ss = mixer_layer(xs, pos_s, PAST_LEN, state_pool[l], state_ret[l], False, norm_gain[l], w_in[l],
                                 pool_w[l], pool_scale[l], ret_gn_gain[l], proj_pool[l], proj_ret[l], w_out[l])
        pool_p.append(hp)
        ret_p.append(Sp.astype(x_prompt.dtype))
        pool_s.append(hs)
        ret_s.append(Ss.astype(state_ret.dtype))
    y_prompt = rms_norm(xp, final_norm)[:, N_META:]
    y_sample = rms_norm(xs, final_norm)
    new_pool_prompt = jnp.stack(pool_p)
    new_ret_prompt = jnp.stack(ret_p)
    new_pool_sample = jnp.stack(pool_s)
    new_ret_sample = jnp.stack(ret_s)
    return (y_prompt, y_sample, new_pool_prompt, new_ret_prompt, new_pool_sample, new_ret_sample)
```

```python
import numpy as np
import ml_dtypes
from contextlib import ExitStack
import concourse.bass as bass
import concourse.mybir as mybir
from concourse.bass_utils import run_bass_kernel_spmd

F32 = mybir.dt.float32
BF16 = mybir.dt.bfloat16
AF = mybir.ActivationFunctionType
ALU = mybir.AluOpType

NCORES = 8
D = 2048
NKC = 16
L = 2
NH = 8
DK = 256
EPS = 1e-6
N_META = 16
SEQ = 2048
NB_S = 16
DEC_SEQ = 8
PAST = 16384
O_U, O_PG, O_Q, O_K, O_V, O_RG, O_GP, O_GR = [i * 2048 for i in range(8)]
KIND_META, KIND_CHUNK, KIND_SAMP = 0, 1, 2
NRING = 4
WB = 256
NSS = 4
SG = 1
NPT = 96 + 72 + 1
PAIRS = [[0, 1], [2, 3], [4, 5], [6, 7]]


_SEM_UID = [0]


class SemObj:
    def __init__(self, nc, es, name):
        self.sem = es.enter_context(nc.semaphore(name))
        self.cnt = 0
        _SEM_UID[0] += 1
        self.uid = _SEM_UID[0]


class Eng(SemObj):
    def __init__(self, nc, es, e, name):
        super().__init__(nc, es, "s_" + name)
        self.e = e
        self.name = name
        self.waited = {}

    def wait(self, tickets):
        best = {}
        for (s, v) in tickets:
            if s is self and v > self.cnt:
                continue
            if self.waited.get(s.uid, 0) >= v:
                continue
            if s.uid not in best or best[s.uid][1] < v:
                best[s.uid] = (s, v)
        for (s, v) in best.values():
            self.e.wait_ge(s.sem, v)
            self.waited[s.uid] = v


class Res:
    __slots__ = ("w", "r")

    def __init__(self):
        self.w = {}
        self.r = {}


def _merge(d, t):
    s, v = t
    if s.uid not in d or d[s.uid][1] < v:
        d[s.uid] = t


def op(eng, fn, reads=(), writes=(), mark=True):
    tk = []
    for R in reads:
        tk += list(R.w.values())
    for W in writes:
        tk += list(W.w.values()) + list(W.r.values())
    eng.wait(tk)
    ins = fn()
    if mark:
        ins.then_inc(eng.sem, 1)
        eng.cnt += 1
        t = (eng, eng.cnt)
    else:
        t = (eng, eng.cnt + 1)
    for R in reads:
        _merge(R.r, t)
    for W in writes:
        W.w = {eng.uid: t}
        W.r = {}
    return t


def dma(q, dsem, out, in_, reads=(), writes=()):
    tk = []
    for R in reads:
        tk += list(R.w.values())
    for W in writes:
        tk += list(W.w.values()) + list(W.r.values())
    q.wait(tk)
    q.e.dma_start(out=out, in_=in_).then_inc(dsem.sem, 16)
    dsem.cnt += 16
    t = (dsem, dsem.cnt)
    for R in reads:
        _merge(R.r, t)
    for W in writes:
        W.w = {dsem.uid: t}
        W.r = {}
    return t


class Slot:
    def __init__(self, t, res=None, dsem=None):
        self.t = t
        self.res = res or Res()
        self.dsem = dsem


class Tile:
    def __init__(self, kind, col, n, gidx, chunk=-1):
        self.kind, self.col, self.n, self.gidx, self.chunk = kind, col, n, gidx, chunk


def make_sts():
    st0 = [Tile(KIND_META, 0, 16, 0)]
    for c in range(8):
        st0.append(Tile(KIND_CHUNK, 16 + 128 * c, 128, 1 + c, c))
    st0.append(Tile(KIND_SAMP, 1040, 128, 9))
    return [st0]


def st_len(tiles):
    return max(t.col + t.n for t in tiles)


def windows(T):
    w = []
    c = 0
    while c < T:
        n = min(512, T - c)
        w.append((c, n))
        c += n
    return w


STS = make_sts()
TMAX = max(st_len(s) for s in STS)
NT_TOTAL = sum(len(s) for s in STS)


def host_consts(role):
    gam = 1.0 - 2.0 ** (-5.0 - np.arange(NH, dtype=np.float64))
    c = {}
    ksc = np.zeros((128, 3, NH), np.float64)
    kd = np.zeros((128, 3, NH), np.float64)
    epsn = np.zeros((128, 3, NH), np.float64)
    dec = np.zeros((128, 3, NH), np.float64)
    for kind, n in ((KIND_META, 16), (KIND_CHUNK, 128), (KIND_SAMP, 8)):
        for p in range(128):
            jc = p % 8 if kind == KIND_SAMP else p
            if kind == KIND_META and p >= 16:
                jc = 15
            ksc[p, kind] = gam ** (-(jc + 1.0)) / 16.0
            kd[p, kind] = gam ** (n - 1.0 - jc) / 16.0 if jc < n else 0.0
            epsn[p, kind] = EPS / (gam ** (jc + 1.0)) ** 2
            dec[p, kind] = gam ** n
    if role == 1:
        kd[:, KIND_META] = 0.0
        dec[:, KIND_META] = 1.0
    kdg = np.zeros((128, 9, NH), np.float64)
    ntok = 1040 if role == 0 else 1024
    for ti in range(9):
        for p in range(128):
            if ti == 0:
                if role == 1 or p >= 16:
                    continue
                tg = p
            else:
                tg = (16 if role == 0 else 0) + 128 * (ti - 1) + p
            kdg[p, ti] = gam ** (ntok - 1.0 - tg) / 16.0
    flag = np.full((128, 1), float(role), np.float64)
    c["ptab"] = np.concatenate([a.reshape(128, -1) for a in (ksc, kd, epsn, dec, kdg, flag)], axis=1).astype(np.float32)
    assert c["ptab"].shape[1] == NPT
    mb = np.zeros((128, 2, 128), np.float32)
    j = np.arange(128)[:, None]
    i = np.arange(128)[None, :]
    mb[:, 0, :] = (i >= j)
    mb[:, 1, :] = ((i // 8) == (j // 8)) & ((i % 8) >= (j % 8))
    c["maskbin"] = mb.astype(ml_dtypes.bfloat16)
    bm = np.zeros((128, 2, 248), np.float32)
    for b in range(2):
        bm[:, b, 120 + 8 * b:120 + 8 * b + 8] = 1.0
    c["bmask"] = bm.astype(ml_dtypes.bfloat16)
    rm = np.zeros((128, NB_S), np.float32)
    for b in range(NB_S):
        rm[8 * b:8 * b + 8, b] = 1.0
    c["rowmask"] = rm.astype(ml_dtypes.bfloat16)
    c["identb"] = np.eye(128, dtype=np.float32).astype(ml_dtypes.bfloat16)
    P = np.zeros((128, 4, 8, 128), np.float64)
    tt = np.arange(128)[None, :]
    tp = np.arange(128)[:, None]
    for g, w in enumerate((2, 4, 8, 16)):
        P[:, g, 0] = ((tt - w < tp) & (tp <= tt)) / w - (tp == tt)
        P[:, g, 1] = (tp > tt + 128 - w) / w
        cntm = np.minimum(w, tt + 1.0)
        P[:, g, 2] = np.where((tt < 16) & (tp < 16), ((tt - w < tp) & (tp <= tt)) / cntm - (tp == tt), 0.0)
        P[:, g, 3] = ((tp < 16) & (tp > tt + 16 - w)) / w
        r, rp = tp % 8, tt % 8
        P[:, g, 4] = np.where(tp // 8 == tt // 8, ((rp - w < r) & (r <= rp)) / w - (r == rp), 0.0)
        bl, sx = tp // 15, tp % 15
        hsel = (tp < 120) & (sx > 15 + rp - w)
        P[:, g, 5] = (hsel & (tt // 8 == bl)) / w
        P[:, g, 6] = (hsel & (tt // 8 == bl + 8)) / w
        if role == 0:
            P[:, g, 7] = 0.0
        else:
            P[:, g, 7] = P[:, g, 1]
            P[:, g, 3] = 0.0
    c["pmat"] = P.astype(np.float32)
    half = 128
    inv_freq = (np.float32(10000.0) ** (-(np.arange(half, dtype=np.float32)) / np.float32(half))).astype(np.float32)
    rope = np.zeros((len(STS), 128, 2, TMAX), np.float32)
    for si, tiles in enumerate(STS):
        pos = np.zeros(TMAX, np.float32)
        for t in tiles:
            if t.kind == KIND_META:
                pos[t.col:t.col + 16] = np.arange(16)
            elif t.kind == KIND_CHUNK:
                pos[t.col:t.col + 128] = N_META + 128 * (8 * role + t.chunk) + np.arange(128)
            else:
                pos[t.col:t.col + 128] = PAST + (np.arange(128) % 8)
        ang = (pos[None, :].astype(np.float32) * inv_freq[:, None]).astype(np.float32)
        rope[si, :, 0, :] = np.cos(ang.astype(np.float64)).astype(np.float32)
        rope[si, :, 1, :] = np.sin(ang.astype(np.float64)).astype(np.float32)
    c["rope"] = rope
    return c


def build_program():
    nc = bass.Bass("TRN2", target_bir_lowering=False)
    es = ExitStack()
    with es:
        _build(nc, es)
    return nc


def _build(nc, es):
    def din(name, shape, dt=F32):
        return nc.dram_tensor(name, list(shape), dt, kind="ExternalInput").ap()

    def dout(name, shape):
        return nc.dram_tensor(name, list(shape), F32, kind="ExternalOutput").ap()

    def dscr(name, shape):
        return nc.dram_tensor(name, list(shape), F32, kind="Internal").ap()

    NST = len(STS)
    xin = din("xin", [NT_TOTAL, 128, D])
    spool = din("spool", [L, NB_S, 15, D])
    sret = din("sret", [L, NB_S, NH, DK, DK])
    w_in = din("w_in", [L, D, 8 * D])
    pool_w = din("pool_w", [L, 4 * 512, 512])
    proj_pool = din("proj_pool", [L, D, D])
    proj_ret = din("proj_ret", [L, D, D])
    w_out = din("w_out", [L, D, D])
    norm_g = din("norm_g", [L + 1, D])
    gn_gain = din("gn_gain", [L, D])
    pscaleT = din("pscaleT", [L, 128, NKC])
    ptab_d = din("ptab", [128, NPT])
    maskbin_d = din("maskbin", [128, 2, 128], BF16)
    bmask_d = din("bmask", [128, 2, 248], BF16)
    rowmask_d = din("rowmask", [128, NB_S], BF16)
    identb_d = din("identb", [128, 128], BF16)
    pmat_d = din("pmat", [128, 4, 8, 128])
    rope_d = din("rope", [NST, 128, 2, TMAX])

    y_out = dout("y", [NT_TOTAL, 128, D])
    npp_out = dout("npp", [L, 15, D])
    nrp_out = dout("nrp", [L, NH, DK, DK])
    nps_out = dout("nps", [L, NB_S, 15, D])
    nrs_out = dout("nrs", [L, NB_S, NH, DK, DK])

    xs1 = dscr("xs1", [NT_TOTAL, 128, D])
    xs2 = dscr("xs2", [NT_TOTAL, 128, D])
    gin_u = dscr("gin_u", [L, 8, 128, WB])
    gout_u = dscr("gout_u", [L, 8, 256, WB])
    gin_s = dscr("gin_s", [L, NH, DK, DK])
    gout_s = dscr("gout_s", [L, NH, 2 * DK, DK])

    PE = Eng(nc, es, nc.tensor, "pe")
    ACT = Eng(nc, es, nc.scalar, "act")
    DVE = Eng(nc, es, nc.vector, "dve")
    POOL = Eng(nc, es, nc.gpsimd, "pool")
    SP = Eng(nc, es, nc.sync, "sp")

    _uid = [0]

    def SB(name, shape, dt, scope=None):
        _uid[0] += 1
        return (scope or es).enter_context(nc.sbuf_tensor(f"sb_{name}_{_uid[0]}", list(shape), dt))

    def PS(name, shape, dt, scope):
        _uid[0] += 1
        return scope.enter_context(nc.psum_tensor(f"ps_{name}_{_uid[0]}", list(shape), dt))

    _dsems = {}
    scope_sems = []

    def dsem(name, scope=None):
        if name not in _dsems:
            _dsems[name] = SemObj(nc, es, "d_" + name)
        if scope is not None and _dsems[name] not in scope_sems:
            scope_sems.append(_dsems[name])
        return _dsems[name]

    def barrier():
        engs = [PE, ACT, DVE, SP]
        tk = [(e, e.cnt) for e in engs] + [(d, d.cnt) for d in scope_sems]
        for e in engs:
            e.wait([t for t in tk if t[0] is not e])
        del scope_sems[:]

    _ncc = [0]

    def exchange(src_ticket, gin_ap, gout_ap):
        _ncc[0] += 1
        cc = SemObj(nc, es, f"cc{_ncc[0]}")
        POOL.wait([src_ticket])
        nc.gpsimd.collective_compute("AllGather", ALU.bypass, replica_groups=PAIRS,
                                     ins=[gin_ap.opt()], outs=[gout_ap.opt()]).then_inc(cc.sem)
        cc.cnt = 1
        return (cc, 1)

    zscr = nc.dram_tensor("zscr", [NKC, 128, TMAX], BF16, kind="Internal").ap()
    ring = [Slot(SB(f"ring{i}", [128, NKC, WB], BF16), dsem=dsem(f"ring{i}")) for i in range(NRING)]
    identb = Slot(SB("identb", [128, 128], BF16), dsem=dsem("c0"))
    ptab = Slot(SB("ptab", [128, NPT], F32), dsem=dsem("c1"))
    maskbin = Slot(SB("maskbin", [128, 2, 128], BF16), dsem=dsem("c2"))
    pscale = Slot(SB("pscale", [128, L, NKC], F32), dsem=dsem("c3"))

    dma(SP, identb.dsem, identb.t[:], identb_d[:, :], writes=[identb.res])
    dma(SP, ptab.dsem, ptab.t[:], ptab_d[:, :], writes=[ptab.res])
    dma(SP, maskbin.dsem, maskbin.t[:], maskbin_d[:, :, :], writes=[maskbin.res])
    for l in range(L):
        dma(SP, pscale.dsem, pscale.t[:, l, :], pscaleT[l], writes=[pscale.res])

    def PT(tab, kind, h):
        c = tab * 24 + kind * 8 + h
        return ptab.t[:, c:c + 1]

    def wblocks():
        seq = []
        for l in range(L):
            for st in range(NST):
                wl = w_in[l]

                def cols(mat, off, k):
                    return mat[:, off + k * WB: off + (k + 1) * WB]
                for k in range(8):
                    seq.append(cols(wl, O_U, k))
                for k in range(2):
                    seq.append(cols(pool_w[l], 0, k))
                for k in range(8):
                    seq.append(cols(wl, O_PG, k))
                for k in range(8):
                    seq.append(cols(wl, O_GP, k))
                    seq.append(cols(proj_pool[l], 0, k))
                for h in range(NH):
                    for off in (O_K, O_V, O_Q, O_RG):
                        seq.append(cols(wl, off, h))
                for k in range(8):
                    seq.append(cols(wl, O_GR, k))
                    seq.append(cols(proj_ret[l], 0, k))
                for k in range(8):
                    seq.append(cols(w_out[l], 0, k))
        return seq

    class WStream:
        def __init__(self):
            self.seq = wblocks()
            self.next_load = 0
            self.next_use = 0
            self.free = list(range(NRING))
            self.assigned = {}
            for _ in range(NRING):
                self.issue()

        def issue(self):
            if self.next_load >= len(self.seq) or not self.free:
                return
            si = self.free.pop(0)
            slot = ring[si]
            sv = self.seq[self.next_load].rearrange("(kc p) n -> p kc n", p=128)
            for q in range(4):
                dma(POOL, slot.dsem, slot.t[:, 4 * q:4 * q + 4, :], sv[:, 4 * q:4 * q + 4, :], writes=[slot.res])
            self.assigned[self.next_load] = si
            self.next_load += 1

        def get(self):
            si = self.assigned.pop(self.next_use)
            self.next_use += 1
            return si

        def release(self, si):
            self.free.append(si)
            self.issue()

    WS = WStream()

    def mm_chain(out, lhs_fn, rhs_fn, reads, wres):
        for kc in range(NKC):
            op(PE, lambda: nc.tensor.matmul(out, lhsT=lhs_fn(kc), rhs=rhs_fn(kc), start=(kc == 0), stop=(kc == NKC - 1)),
               reads=reads, writes=[wres], mark=(kc == NKC - 1))

    for l in range(L):
        src_x = xin if l == 0 else xs1
        dst_x = xs1 if l == 0 else xs2
        for st, tiles in enumerate(STS):
            T = st_len(tiles)
            wins = windows(T)
            ntl = len(tiles)
            Hres = [Res() for _ in tiles]
            Xres = [[Res() for _ in tiles] for _ in range(NKC)]
            Yres = [[Res() for _ in tiles] for _ in range(NKC)]
            has_samp = any(t.kind == KIND_SAMP for t in tiles)

            def tiles_in(c0, n):
                return [i for i, t in enumerate(tiles) if t.col < c0 + n and t.col + t.n > c0]

            big = ExitStack()
            H = SB("H", [128, NKC, TMAX], BF16, big)
            Y = SB("Y", [128, NKC, TMAX], BF16, big)
            zres = [[Res() for _ in wins] for _ in range(NKC)]

            with ExitStack() as sc:
                xt = [Slot(SB(f"xt{i}", [128, D], F32, sc), dsem=dsem(f"xt{i}", sc)) for i in range(2)]
                gb = Slot(SB("gb", [128, D], F32, sc), dsem=dsem("gb", sc))
                hb = [Slot(SB(f"hb{i}", [128, D], BF16, sc)) for i in range(2)]
                junk = Slot(SB("junk", [128, D], BF16, sc))
                ss = [Slot(SB(f"ss{i}", [128, 1], F32, sc)) for i in range(2)]
                rs = [Slot(SB(f"rs{i}", [128, 1], F32, sc)) for i in range(2)]
                pT = [Slot(PS(f"pT{i}", [128, D], BF16, sc)) for i in range(2)]
                dma(SP, gb.dsem, gb.t[:], norm_g[l:l + 1, :].partition_broadcast(128), writes=[gb.res])
                dma(SP, xt[0].dsem, xt[0].t[:], src_x[tiles[0].gidx], writes=[xt[0].res])
                for i, t in enumerate(tiles):
                    b = i % 2
                    if i + 1 < ntl:
                        dma(SP, xt[1 - b].dsem, xt[1 - b].t[:], src_x[tiles[i + 1].gidx], writes=[xt[1 - b].res])
                    op(ACT, lambda: nc.scalar.activation(out=junk.t[:], in_=xt[b].t[:], func=AF.Square,
                                                         accum_out=ss[b].t[:]),
                       reads=[xt[b].res], writes=[junk.res, ss[b].res])
                    op(ACT, lambda: nc.scalar.activation(out=ss[b].t[:], in_=ss[b].t[:], func=AF.Sqrt,
                                                         bias=EPS, scale=1.0 / D),
                       reads=[ss[b].res], writes=[ss[b].res])
                    op(DVE, lambda: nc.vector.reciprocal(out=rs[b].t[:], in_=ss[b].t[:]),
                       reads=[ss[b].res], writes=[rs[b].res])
                    op(DVE, lambda: nc.vector.scalar_tensor_tensor(out=hb[b].t[:], in0=xt[b].t[:], scalar=rs[b].t[:, 0:1],
                                                                   in1=gb.t[:], op0=ALU.mult, op1=ALU.mult),
                       reads=[xt[b].res, rs[b].res, gb.res], writes=[hb[b].res])
                    for kc in range(NKC):
                        op(PE, lambda: nc.tensor.transpose(pT[b].t[:, kc * 128:(kc + 1) * 128],
                                                           hb[b].t[:, kc * 128:(kc + 1) * 128], identb.t[:]),
                           reads=[hb[b].res, identb.res], writes=[pT[b].res], mark=(kc == NKC - 1))
                    op(ACT, lambda: nc.scalar.copy(out=H[:, :, t.col:t.col + t.n],
                                                   in_=pT[b].t[:].rearrange("p (k c) -> p k c", c=128)[:, :, 0:t.n]),
                       reads=[pT[b].res], writes=[Hres[i]])
                barrier()

            with ExitStack() as sc:
                X = SB("X", [128, NKC, TMAX], BF16, sc)
                zst = dsem("zst", sc)
                pm = Slot(SB("pm", [128, 4, 8, 128], F32, sc), dsem=dsem("pm", sc))
                dma(SP, pm.dsem, pm.t[:], pmat_d[:, :, :, :], writes=[pm.res])
                ut = [Slot(SB(f"ut{i}", [128, WB], F32, sc), dsem=dsem(f"ut{i}", sc)) for i in range(3)]
                hist = [Slot(SB(f"hist{i}", [120, WB], F32, sc), dsem=dsem(f"hist{i}", sc)) for i in range(2)]
                upv = Slot(SB("upv", [128, WB], F32, sc), dsem=dsem("upv", sc))
                sact = [Slot(SB(f"sact{i}", [128, 512], F32, sc)) for i in range(2)]
                pa = [Slot(PS(f"pa{i}", [128, 512], F32, sc)) for i in range(2)]
                pb = [Slot(PS(f"pb{i}", [128, 512], F32, sc)) for i in range(2)]
                hcp_sem = dsem("hcp", sc)

                if has_samp:
                    dma(SP, hcp_sem, nps_out[l, :, 0:7, :], spool[l, :, 8:15, :])
                utL = Slot(SB("utL", [128, WB], F32, sc), dsem=dsem("utL", sc))
                utF2 = [Slot(SB(f"utF{i}", [128, WB], F32, sc)) for i in range(2)]
                utM2 = [Slot(SB(f"utM{i}", [128, WB], F32, sc)) for i in range(2)]
                urecv2 = [Slot(SB(f"urecv{i}", [128, WB], F32, sc), dsem=dsem(f"urecv{i}", sc)) for i in range(2)]
                deferred = []
                ncc = WB // 128
                chunk_tiles = [i for i, t in enumerate(tiles) if t.kind == KIND_CHUNK]
                i_first, i_last = chunk_tiles[0], chunk_tiles[-1]
                cnt = 0
                rot = 0
                for hg in range(8):
                    g = hg // 2
                    ccol = slice(hg * WB, (hg + 1) * WB)
                    si = WS.get()
                    wslot = ring[si]
                    xc0 = hg * ncc
                    utF, utM, urecv = utF2[hg % 2], utM2[hg % 2], urecv2[hg % 2]

                    def u_mm(i, t, dst, b):
                        mm_chain(pa[b].t[0:t.n, 0:WB], lambda kc: H[:, kc, t.col:t.col + t.n], lambda kc: wslot.t[:, kc, :],
                                 [Hres[i], wslot.res], pa[b].res)
                        op(ACT, lambda: nc.scalar.copy(out=dst.t[0:t.n, :], in_=pa[b].t[0:t.n, 0:WB]),
                           reads=[pa[b].res], writes=[dst.res])

                    def pool_mm(i, t, terms, b, g=g, xc0=xc0):
                        for cc in range(ncc):
                            for ti, (ls, K, ptype) in enumerate(terms):
                                op(PE, lambda: nc.tensor.matmul(pb[b].t[:, cc * 128:cc * 128 + t.n],
                                                                lhsT=ls.t[0:K, cc * 128:(cc + 1) * 128],
                                                                rhs=pm.t[0:K, g, ptype, 0:t.n],
                                                                start=(ti == 0), stop=(ti == len(terms) - 1)),
                                   reads=[ls.res, pm.res], writes=[pb[b].res],
                                   mark=(cc == ncc - 1 and ti == len(terms) - 1))
                        op(DVE, lambda: nc.vector.tensor_copy(
                            out=X[:, xc0:xc0 + ncc, t.col:t.col + t.n],
                            in_=pb[b].t[:, 0:ncc * 128].rearrange("p (c n) -> p c n", n=128)[:, :, 0:t.n]),
                           reads=[pb[b].res], writes=[Xres[xc0 + c][i] for c in range(ncc)])

                    if has_samp:
                        for k in range(2):
                            dma(SP, hist[k].dsem, hist[k].t[:],
                                spool[l, 8 * k:8 * k + 8, :, ccol].rearrange("b s c -> (b s) c"),
                                writes=[hist[k].res])
                    tL = tiles[i_last]
                    b = cnt % 2
                    cnt += 1
                    u_mm(i_last, tL, utL, b)
                    dma(SP, utL.dsem, npp_out[l, :, ccol], utL.t[113:128, :], reads=[utL.res])
                    tk_g = dma(SP, utL.dsem, gin_u[l, hg], utL.t[:], reads=[utL.res])
                    tk_cc = exchange(tk_g, gin_u[l, hg], gout_u[l, hg])
                    prev = None
                    bF = None
                    for i, t in enumerate(tiles):
                        if i == i_last:
                            continue
                        b = cnt % 2
                        cnt += 1
                        if t.kind == KIND_META:
                            dst = utM
                        elif i == i_first:
                            dst = utF
                        else:
                            dst = ut[rot % 3]
                            rot += 1
                        u_mm(i, t, dst, b)
                        if t.kind == KIND_SAMP:
                            for bq in range(NB_S):
                                dma(SP, dst.dsem, nps_out[l, bq, 7:15, ccol], dst.t[8 * bq:8 * bq + 8, :], reads=[dst.res])
                            pool_mm(i, t, [(dst, 128, 4), (hist[0], 120, 5), (hist[1], 120, 6)], b)
                            continue
                        if t.kind == KIND_META:
                            pool_mm(i, t, [(dst, 16, 2)], b)
                        elif i == i_first:
                            pass
                        else:
                            pool_mm(i, t, [(dst, 128, 0), (prev[0], 128, 1)], b)
                        prev = (dst, t)
                    b = cnt % 2
                    cnt += 1
                    pool_mm(i_last, tL, [(utL, 128, 0), (prev[0], 128, 1)], b)
                    def first_chunk(hg=hg, tk_cc=tk_cc, utF=utF, utM=utM, urecv=urecv, pool_mm=pool_mm):
                        SP.wait([tk_cc])
                        dma(SP, urecv.dsem, urecv.t[:], gout_u[l, hg][0:128, :], writes=[urecv.res])
                        pool_mm(i_first, tiles[i_first], [(utF, 128, 0), (utM, 16, 3), (urecv, 128, 7)], 0)
                    for fn in deferred:
                        fn()
                    deferred[:] = [first_chunk]
                    WS.release(si)
                for fn in deferred:
                    fn()

                si_pw = [WS.get() for _ in range(512 // WB)]
                pw = [ring[s_] for s_ in si_pw]
                cnt = 0
                npb = WB // 128
                for k in range(2048 // WB):
                    si = WS.get()
                    wslot = ring[si]
                    for dl in range(npb):
                        dc = k * npb + dl
                        g = dc // 4
                        dcl = dc % 4
                        pwb = pw[(dcl * 128) // WB]
                        pwo = (dcl * 128) % WB
                        for (c0, n) in wins:
                            b = cnt % 2
                            cnt += 1
                            til = tiles_in(c0, n)
                            mm_chain(pa[b].t[:, 0:n], lambda kc: wslot.t[:, kc, dl * 128:(dl + 1) * 128],
                                     lambda kc: H[:, kc, c0:c0 + n], [wslot.res] + [Hres[i] for i in til], pa[b].res)
                            op(ACT, lambda: nc.scalar.activation(out=sact[b].t[:, 0:n], in_=pa[b].t[:, 0:n], func=AF.Silu),
                               reads=[pa[b].res], writes=[sact[b].res])
                            for c4 in range(4):
                                op(PE, lambda: nc.tensor.matmul(pb[b].t[:, 0:n], lhsT=pwb.t[:, 4 * g + c4, pwo:pwo + 128],
                                                                rhs=X[:, 4 * g + c4, c0:c0 + n], start=(c4 == 0), stop=(c4 == 3)),
                                   reads=[pwb.res] + [Xres[4 * g + c4][i] for i in til], writes=[pb[b].res], mark=(c4 == 3))
                            op(DVE, lambda: nc.vector.scalar_tensor_tensor(out=Y[:, dc, c0:c0 + n], in0=pb[b].t[:, 0:n],
                                                                           scalar=pscale.t[:, l, dc:dc + 1], in1=sact[b].t[:, 0:n],
                                                                           op0=ALU.mult, op1=ALU.mult),
                               reads=[pb[b].res, sact[b].res, pscale.res], writes=[Yres[dc][i] for i in til])
                    WS.release(si)
                for s_ in si_pw:
                    WS.release(s_)

                for k in range(2048 // WB):
                    si_g = WS.get()
                    si_p = WS.get()
                    wg, wp = ring[si_g], ring[si_p]
                    for fl in range(npb):
                        f = k * npb + fl
                        for (c0, n) in wins:
                            b = cnt % 2
                            cnt += 1
                            til = tiles_in(c0, n)
                            mm_chain(pa[b].t[:, 0:n], lambda kc: wg.t[:, kc, fl * 128:(fl + 1) * 128],
                                     lambda kc: H[:, kc, c0:c0 + n], [wg.res] + [Hres[i] for i in til], pa[b].res)
                            op(ACT, lambda: nc.scalar.activation(out=sact[b].t[:, 0:n], in_=pa[b].t[:, 0:n], func=AF.Sigmoid),
                               reads=[pa[b].res], writes=[sact[b].res])
                            mm_chain(pb[b].t[:, 0:n], lambda kc: wp.t[:, kc, fl * 128:(fl + 1) * 128],
                                     lambda kc: Y[:, kc, c0:c0 + n],
                                     [wp.res] + [Yres[kc][i] for kc in range(NKC) for i in til], pb[b].res)
                            op(DVE, lambda: nc.vector.tensor_tensor(out=X[:, f, c0:c0 + n], in0=pb[b].t[:, 0:n],
                                                                    in1=sact[b].t[:, 0:n], op=ALU.mult),
                               reads=[pb[b].res, sact[b].res], writes=[Xres[f][i] for i in til])
                            wi = wins.index((c0, n))
                            dma(SP, zst, zscr[f][:, c0:c0 + n], X[:, f, c0:c0 + n],
                                reads=[Xres[f][i] for i in til], writes=[zres[f][wi]])
                    WS.release(si_g)
                    WS.release(si_p)
                barrier()

            with ExitStack() as sc:
                rope = Slot(SB("rope", [128, 2, TMAX], F32, sc), dsem=dsem("rope", sc))
                dma(SP, rope.dsem, rope.t[:], rope_d[st], writes=[rope.res])
                prompt_tiles = [i for i, t in enumerate(tiles) if t.kind != KIND_SAMP]
                PH = []
                for par in range(2):
                    PH.append(dict(
                        qT=Slot(SB(f"qT{par}", [128, 2, TMAX], BF16, sc)),
                        kT=Slot(SB(f"kT{par}", [128, 2, TMAX], BF16, sc)),
                        vS=[Slot(SB(f"v{par}_{i}", [128, 256], BF16, sc)) for i in range(ntl)],
                        GS=[Slot(SB(f"G{par}_{i}", [128, 256], BF16, sc)) for i in range(ntl)],
                        kt=[Slot(SB(f"kt{par}_{i}", [128, 256], BF16, sc)) for i in range(ntl)],
                        gnb=Slot(SB(f"gnb{par}", [128, 256], F32, sc), dsem=dsem(f"gnb{par}", sc)),
                    ))
                khat = [Slot(SB(f"kh{i}", [128, 256], BF16, sc)) for i in prompt_tiles]
                tmp = [Slot(SB(f"tmp{i}", [128, 512], F32, sc)) for i in range(2)]
                tmp[0].dsem = dsem("Send", sc)
                Sf = Slot(SB("Sf", [128, 2, 256], F32, sc), dsem=dsem("Sf", sc))
                st6 = [Slot(SB(f"st6{i}", [128, 6], F32, sc)) for i in range(2)]
                mv = [Slot(SB(f"mv{i}", [128, 2], F32, sc)) for i in range(2)]
                rsd = [Slot(SB(f"rsd{i}", [128, 1], F32, sc)) for i in range(2)]
                on = [Slot(SB("on0", [128, 256], F32, sc))] * 2
                rb = [Slot(SB(f"rb{i}", [128, 256], BF16, sc)) for i in range(3)]
                sr = [Slot(SB("sr0", [128, 256], F32, sc))] * 2
                if has_samp:
                    bmask = Slot(SB("bmask", [128, 2, 248], BF16, sc), dsem=dsem("bmask", sc))
                    rowmask = Slot(SB("rowmask", [128, NB_S], BF16, sc), dsem=dsem("rowmask", sc))
                    dma(SP, bmask.dsem, bmask.t[:], bmask_d[:, :, :], writes=[bmask.res])
                    dma(SP, rowmask.dsem, rowmask.t[:], rowmask_d[:, :], writes=[rowmask.res])
                    QS = 2
                    qmB = [Slot(SB(f"qm{i}", [128, 2, QS, 128], BF16, sc)) for i in range(2)]
                    ktmB = [Slot(SB(f"ktm{i}", [128, QS, 256], BF16, sc)) for i in range(2)]
                    Ss = [Slot(SB(f"Ss{i}", [128, SG, 2, 256], F32, sc), dsem=dsem(f"Ss{i}", sc)) for i in range(NSS)]
                    Ssb = [Slot(SB(f"Ssb{i}", [128, SG, 2, 256], BF16, sc)) for i in range(2)]
                pA = [Slot(PS(f"pA{i}", [128, 512], F32, sc)) for i in range(4)]
                pX = Slot(PS("pX", [128, 1024], BF16, sc))
                pA3 = pA[3]
                bB = [Slot(PS(f"bB{i}", [128, 512], F32, sc)) for i in range(3)]
                scA = [Slot(SB(f"scA{i}", [128, 128], BF16, sc)) for i in range(ntl)]
                Sball = [Slot(SB(f"Sball{i}", [128, 2, 256], BF16, sc)) for i in prompt_tiles]
                sgrp = [0]
                cc_tk = {}

                def genA(h):
                    P = PH[h % 2]
                    qT, kT, vS, GS, kt, gnb = P["qT"], P["kT"], P["vS"], P["GS"], P["kt"], P["gnb"]
                    dma(SP, gnb.dsem, gnb.t[:], gn_gain[l:l + 1, h * 256:(h + 1) * 256].partition_broadcast(128),
                        writes=[gnb.res])
                    for dst in (kT,):
                        si_w = WS.get()
                        wq = ring[si_w]
                        for wi_, (c0, n) in enumerate(wins):
                            til = tiles_in(c0, n)
                            pp = (pA[2 * (wi_ % 2)], pA[2 * (wi_ % 2) + 1])
                            for j in range(2):
                                mm_chain(pp[j].t[:, 0:n], lambda kc: wq.t[:, kc, j * 128:(j + 1) * 128],
                                         lambda kc: H[:, kc, c0:c0 + n], [wq.res] + [Hres[i] for i in til], pp[j].res)
                                if j == 0:
                                    yield
                            cs = rope.t[:, 0, c0:c0 + n]
                            sn = rope.t[:, 1, c0:c0 + n]
                            p0, p1 = pp
                            for (half, pa_, pb_, opx) in ((0, p0, p1, ALU.subtract), (1, p1, p0, ALU.add)):
                                op(DVE, lambda: nc.vector.tensor_tensor(out=tmp[0].t[:, 0:n], in0=pa_.t[:, 0:n], in1=cs, op=ALU.mult),
                                   reads=[pa_.res, rope.res], writes=[tmp[0].res])
                                op(DVE, lambda: nc.vector.tensor_tensor(out=tmp[1].t[:, 0:n], in0=pb_.t[:, 0:n], in1=sn, op=ALU.mult),
                                   reads=[pb_.res, rope.res], writes=[tmp[1].res])
                                op(DVE, lambda: nc.vector.tensor_tensor(out=dst.t[:, half, c0:c0 + n], in0=tmp[0].t[:, 0:n],
                                                                        in1=tmp[1].t[:, 0:n], op=opx),
                                   reads=[tmp[0].res, tmp[1].res], writes=[dst.res])
                            yield
                        WS.release(si_w)
                    si_v = WS.get()
                    wv = ring[si_v]
                    for i, t in enumerate(tiles):
                        b = i % 4
                        mm_chain(pA[b].t[0:t.n, 0:256], lambda kc: H[:, kc, t.col:t.col + t.n], lambda kc: wv.t[:, kc, :],
                                 [wv.res, Hres[i]], pA[b].res)
                        op(ACT, lambda: nc.scalar.copy(out=vS[i].t[0:t.n, :], in_=pA[b].t[0:t.n, 0:256]),
                           reads=[pA[b].res], writes=[vS[i].res])
                        yield
                    WS.release(si_v)
                    for i, t in enumerate(tiles):
                        hf = 512 + (i % 2) * 256
                        for cc in range(2):
                            op(PE, lambda: nc.tensor.transpose(pX.t[0:t.n, hf + cc * 128:hf + (cc + 1) * 128],
                                                               kT.t[:, cc, t.col:t.col + t.n], identb.t[:]),
                               reads=[kT.res, identb.res], writes=[pX.res], mark=(cc == 1))
                        op(ACT, lambda: nc.scalar.activation(out=kt[i].t[0:t.n, :], in_=pX.t[0:t.n, hf:hf + 256], func=AF.Copy,
                                                             scale=PT(1, t.kind, h)[0:t.n, :]),
                           reads=[pX.res, ptab.res], writes=[kt[i].res])
                        if i in prompt_tiles:
                            pi = prompt_tiles.index(i)
                            kcol = 96 + pi * 8 + h
                            op(ACT, lambda: nc.scalar.activation(out=khat[pi].t[0:t.n, :], in_=pX.t[0:t.n, hf:hf + 256],
                                                                 func=AF.Copy, scale=ptab.t[0:t.n, kcol:kcol + 1]),
                               reads=[pX.res, ptab.res], writes=[khat[pi].res])
                        yield
                    for cc in range(2):
                        for pi, i in enumerate(prompt_tiles):
                            t = tiles[i]
                            op(PE, lambda: nc.tensor.matmul(pA3.t[:, cc * 256:(cc + 1) * 256],
                                                            lhsT=khat[pi].t[0:t.n, cc * 128:(cc + 1) * 128], rhs=vS[i].t[0:t.n, :],
                                                            start=(pi == 0), stop=(pi == len(prompt_tiles) - 1)),
                               reads=[khat[pi].res, vS[i].res], writes=[pA3.res],
                               mark=(pi == len(prompt_tiles) - 1))
                        yield
                    Send = tmp[0]
                    op(ACT, lambda: nc.scalar.copy(out=Send.t[:, :], in_=pA3.t[:, :]),
                       reads=[pA3.res], writes=[Send.res])
                    tk_g = dma(SP, Send.dsem, gin_s[l, h].rearrange("(c p) e -> p c e", p=128),
                               Send.t[:, :].rearrange("p (c e) -> p c e", c=2), reads=[Send.res])
                    cc_tk[h] = exchange(tk_g, gin_s[l, h], gout_s[l, h])

                    for dst in (qT,):
                        si_w = WS.get()
                        wq = ring[si_w]
                        for wi_, (c0, n) in enumerate(wins):
                            til = tiles_in(c0, n)
                            pp = (pA[2 * (wi_ % 2)], pA[2 * (wi_ % 2) + 1])
                            for j in range(2):
                                mm_chain(pp[j].t[:, 0:n], lambda kc: wq.t[:, kc, j * 128:(j + 1) * 128],
                                         lambda kc: H[:, kc, c0:c0 + n], [wq.res] + [Hres[i] for i in til], pp[j].res)
                                if j == 0:
                                    yield
                            cs = rope.t[:, 0, c0:c0 + n]
                            sn = rope.t[:, 1, c0:c0 + n]
                            p0, p1 = pp
                            for (half, pa_, pb_, opx) in ((0, p0, p1, ALU.subtract), (1, p1, p0, ALU.add)):
                                op(DVE, lambda: nc.vector.tensor_tensor(out=tmp[0].t[:, 0:n], in0=pa_.t[:, 0:n], in1=cs, op=ALU.mult),
                                   reads=[pa_.res, rope.res], writes=[tmp[0].res])
                                op(DVE, lambda: nc.vector.tensor_tensor(out=tmp[1].t[:, 0:n], in0=pb_.t[:, 0:n], in1=sn, op=ALU.mult),
                                   reads=[pb_.res, rope.res], writes=[tmp[1].res])
                                op(DVE, lambda: nc.vector.tensor_tensor(out=dst.t[:, half, c0:c0 + n], in0=tmp[0].t[:, 0:n],
                                                                        in1=tmp[1].t[:, 0:n], op=opx),
                                   reads=[tmp[0].res, tmp[1].res], writes=[dst.res])
                            yield
                        WS.release(si_w)
                    si_r = WS.get()
                    wr = ring[si_r]
                    for i, t in enumerate(tiles):
                        b = i % 4
                        b2 = i % 2
                        mm_chain(pA[b].t[0:t.n, 0:256], lambda kc: H[:, kc, t.col:t.col + t.n], lambda kc: wr.t[:, kc, :],
                                 [wr.res, Hres[i]], pA[b].res)
                        op(ACT, lambda: nc.scalar.activation(out=sr[b2].t[0:t.n, :], in_=pA[b].t[0:t.n, 0:256], func=AF.Silu),
                           reads=[pA[b].res], writes=[sr[b2].res])
                        op(DVE, lambda: nc.vector.tensor_tensor(out=GS[i].t[0:t.n, :], in0=sr[b2].t[0:t.n, :], in1=gnb.t[0:t.n, :],
                                                                op=ALU.mult),
                           reads=[sr[b2].res, gnb.res], writes=[GS[i].res])
                        yield
                    WS.release(si_r)
                def genB(h):
                    P = PH[h % 2]
                    qT, kT, vS, GS, kt = P["qT"], P["kT"], P["vS"], P["GS"], P["kt"]
                    npr = len(prompt_tiles)
                    i_samp = [i for i, t in enumerate(tiles) if t.kind == KIND_SAMP]
                    groups = [(hb_, gq) for hb_ in range(NB_S // QS) for gq in range(QS // SG)] if i_samp else []

                    def ss_load(gi):
                        hb_, gq = groups[gi]
                        sb_ = (sgrp[0] + gi) % NSS
                        b0 = QS * hb_ + SG * gq
                        for bl in range(SG):
                            dma(SP, Ss[sb_].dsem, Ss[sb_].t[:, bl], sret[l, b0 + bl, h].rearrange("(c p) e -> p c e", p=128),
                                writes=[Ss[sb_].res] if bl == 0 else [])
                        Ss[sb_].res.w = {Ss[sb_].dsem.uid: (Ss[sb_].dsem, Ss[sb_].dsem.cnt)}

                    def ss_cast(gi):
                        s3_, s2_ = (sgrp[0] + gi) % NSS, (sgrp[0] + gi) % 2
                        op(ACT, lambda: nc.scalar.copy(out=Ssb[s2_].t[:], in_=Ss[s3_].t[:]),
                           reads=[Ss[s3_].res], writes=[Ssb[s2_].res])

                    if groups:
                        ss_load(0)
                        ss_load(1)
                    SP.wait([cc_tk[h]])
                    dma(SP, Sf.dsem, Sf.t[:], gout_s[l, h][0:DK, :].rearrange("(c p) e -> p c e", p=128), writes=[Sf.res])
                    op(DVE, lambda: nc.vector.tensor_scalar(out=Sf.t[:].rearrange("p c e -> p (c e)"),
                                                            in0=Sf.t[:].rearrange("p c e -> p (c e)"),
                                                            scalar1=ptab.t[:, NPT - 1:NPT], scalar2=None, op0=ALU.mult),
                       reads=[Sf.res, ptab.res], writes=[Sf.res])
                    op(ACT, lambda: nc.scalar.copy(out=Sball[0].t[:], in_=Sf.t[:]), reads=[Sf.res], writes=[Sball[0].res])
                    yield

                    def scores(i):
                        t = tiles[i]
                        n, c0, kind = t.n, t.col, t.kind
                        mk = 1 if kind == KIND_SAMP else 0
                        for cc in range(2):
                            op(PE, lambda: nc.tensor.matmul(bB[0].t[0:n, 0:n], lhsT=kT.t[:, cc, c0:c0 + n], rhs=qT.t[:, cc, c0:c0 + n],
                                                            start=(cc == 0), stop=(cc == 1)),
                               reads=[kT.res, qT.res], writes=[bB[0].res], mark=(cc == 1))
                        op(DVE, lambda: nc.vector.scalar_tensor_tensor(out=scA[i].t[0:n, 0:n], in0=bB[0].t[0:n, 0:n],
                                                                       scalar=PT(0, kind, h)[0:n, :],
                                                                       in1=maskbin.t[0:n, mk, 0:n], op0=ALU.mult, op1=ALU.mult),
                           reads=[bB[0].res, ptab.res, maskbin.res], writes=[scA[i].res])

                    for pi, i in enumerate(prompt_tiles):
                        t = tiles[i]
                        n, kind = t.n, t.kind
                        scores(i)
                        pin = bB[1 + pi % 2]
                        for cc in range(2):
                            op(PE, lambda: nc.tensor.matmul(pin.t[:, cc * 256:(cc + 1) * 256],
                                                            lhsT=kt[i].t[0:n, cc * 128:(cc + 1) * 128], rhs=vS[i].t[0:n, :],
                                                            start=True, stop=True),
                               reads=[kt[i].res, vS[i].res], writes=[pin.res], mark=(cc == 1))
                        op(DVE, lambda: nc.vector.scalar_tensor_tensor(
                            out=Sf.t[:].rearrange("p c e -> p (c e)"), in0=Sf.t[:].rearrange("p c e -> p (c e)"),
                            scalar=PT(3, kind, h), in1=pin.t[:, :], op0=ALU.mult, op1=ALU.add),
                           reads=[pin.res, Sf.res, ptab.res], writes=[Sf.res])
                        if pi + 1 < npr:
                            op(ACT, lambda: nc.scalar.copy(out=Sball[pi + 1].t[:], in_=Sf.t[:]),
                               reads=[Sf.res], writes=[Sball[pi + 1].res])
                        yield
                    dma(SP, Sf.dsem, nrp_out[l, h].rearrange("(c p) e -> p c e", p=128), Sf.t[:], reads=[Sf.res])
                    for i in i_samp:
                        scores(i)
                        yield

                    def norm_chain(i, po):
                        t = tiles[i]
                        n, kind, p = t.n, t.kind, i % 2
                        op(DVE, lambda: nc.vector.bn_stats(out=st6[p].t[0:n, :], in_=po.t[0:n, 0:256]),
                           reads=[po.res], writes=[st6[p].res])
                        op(DVE, lambda: nc.vector.bn_aggr(out=mv[p].t[0:n, :], in_=st6[p].t[0:n, :]),
                           reads=[st6[p].res], writes=[mv[p].res])
                        op(ACT, lambda: nc.scalar.activation(out=rsd[p].t[0:n, :], in_=mv[p].t[0:n, 1:2], func=AF.Sqrt,
                                                             bias=PT(2, kind, h)[0:n, :], scale=1.0),
                           reads=[mv[p].res, ptab.res], writes=[rsd[p].res])
                        op(DVE, lambda: nc.vector.reciprocal(out=rsd[p].t[0:n, :], in_=rsd[p].t[0:n, :]),
                           reads=[rsd[p].res], writes=[rsd[p].res])
                        op(DVE, lambda: nc.vector.tensor_scalar(out=on[p].t[0:n, :], in0=po.t[0:n, 0:256], scalar1=mv[p].t[0:n, 0:1],
                                                                scalar2=rsd[p].t[0:n, 0:1], op0=ALU.subtract, op1=ALU.mult),
                           reads=[po.res, mv[p].res, rsd[p].res], writes=[on[p].res])
                        op(DVE, lambda: nc.vector.tensor_tensor(out=rb[i % 3].t[0:n, :], in0=on[p].t[0:n, :], in1=GS[i].t[0:n, :],
                                                                op=ALU.mult),
                           reads=[on[p].res, GS[i].res], writes=[rb[i % 3].res])

                    def S3(i):
                        t = tiles[i]
                        n, c0, p = t.n, t.col, i % 3
                        for ec in range(2):
                            op(PE, lambda: nc.tensor.transpose(pX.t[:, ec * 128:ec * 128 + n], rb[p].t[0:n, ec * 128:(ec + 1) * 128],
                                                               identb.t[0:n, 0:n]),
                               reads=[rb[p].res, identb.res], writes=[pX.res], mark=(ec == 1))
                        op(ACT, lambda: nc.scalar.copy(out=Y[:, 2 * h:2 * h + 2, c0:c0 + n],
                                                       in_=pX.t[:, 0:256].rearrange("p (c n) -> p c n", n=128)[:, :, 0:n]),
                           reads=[pX.res], writes=[Yres[2 * h][i], Yres[2 * h + 1][i]])

                    pend = []
                    for pi, i in enumerate(prompt_tiles):
                        t = tiles[i]
                        n, c0 = t.n, t.col
                        po = bB[1 + pi % 2]
                        op(PE, lambda: nc.tensor.matmul(po.t[0:n, 0:256], lhsT=scA[i].t[0:n, 0:n], rhs=vS[i].t[0:n, :],
                                                        start=True, stop=False),
                           reads=[scA[i].res, vS[i].res], writes=[po.res], mark=False)
                        for cc in range(2):
                            op(PE, lambda: nc.tensor.matmul(po.t[0:n, 0:256], lhsT=qT.t[:, cc, c0:c0 + n], rhs=Sball[pi].t[:, cc, :],
                                                            start=False, stop=(cc == 1)),
                               reads=[qT.res, Sball[pi].res], writes=[po.res], mark=(cc == 1))
                        norm_chain(i, po)
                        pend.append(i)
                        yield
                        if len(pend) > 2:
                            S3(pend.pop(0))
                            yield
                    for i in i_samp:
                        t = tiles[i]
                        c0 = t.col
                        po = bB[1 + npr % 2]
                        pinc = [bB[0], bB[1 + (npr + 1) % 2]]

                        def masks(hb_):
                            qm, ktm = qmB[hb_ % 2], ktmB[hb_ % 2]
                            for cc in range(2):
                                op(DVE, lambda: nc.vector.tensor_tensor(
                                    out=qm.t[:, cc], in0=qT.t[:, cc, c0:c0 + 128].unsqueeze(1).to_broadcast([128, QS, 128]),
                                    in1=bmask.t[:, :, 120 - 16 * hb_:248 - 16 * hb_], op=ALU.mult),
                                   reads=[qT.res, bmask.res], writes=[qm.res])
                            op(DVE, lambda: nc.vector.tensor_tensor(
                                out=ktm.t[:], in0=kt[i].t[:].unsqueeze(1).to_broadcast([128, QS, 256]),
                                in1=rowmask.t[:, QS * hb_:QS * hb_ + QS].unsqueeze(2).to_broadcast([128, QS, 256]), op=ALU.mult),
                               reads=[kt[i].res, rowmask.res], writes=[ktm.res])

                        masks(0)
                        ss_cast(0)
                        op(PE, lambda: nc.tensor.matmul(po.t[:, 0:256], lhsT=scA[i].t[:, :], rhs=vS[i].t[:, :],
                                                        start=True, stop=False),
                           reads=[scA[i].res, vS[i].res], writes=[po.res], mark=False)
                        for gi, (hb_, gq) in enumerate(groups):
                            s3_, s2_ = (sgrp[0] + gi) % NSS, (sgrp[0] + gi) % 2
                            qm, ktm = qmB[hb_ % 2], ktmB[hb_ % 2]
                            b0 = QS * hb_ + SG * gq
                            if gi + 2 < len(groups):
                                ss_load(gi + 2)
                            if gi + 1 < len(groups):
                                ss_cast(gi + 1)
                                if groups[gi + 1][0] != hb_:
                                    masks(groups[gi + 1][0])
                            for bl in range(SG):
                                bh = SG * gq + bl
                                for cc in range(2):
                                    last = (gi == len(groups) - 1 and bl == SG - 1 and cc == 1)
                                    op(PE, lambda: nc.tensor.matmul(po.t[:, 0:256], lhsT=qm.t[:, cc, bh, :],
                                                                    rhs=Ssb[s2_].t[:, bl, cc, :], start=False, stop=last),
                                       reads=[qm.res, Ssb[s2_].res], writes=[po.res],
                                       mark=(last or (bl == SG - 1 and cc == 1)))
                            for bl in range(SG):
                                bh = SG * gq + bl
                                ib = (gi * SG + bl) % 2
                                for cc in range(2):
                                    op(PE, lambda: nc.tensor.matmul(pinc[ib].t[:, cc * 256:(cc + 1) * 256],
                                                                    lhsT=ktm.t[:, bh, cc * 128:(cc + 1) * 128],
                                                                    rhs=vS[i].t[:, :], start=True, stop=True),
                                       reads=[ktm.res, vS[i].res], writes=[pinc[ib].res], mark=(cc == 1))
                                op(DVE, lambda: nc.vector.scalar_tensor_tensor(
                                    out=Ss[s3_].t[:, bl].rearrange("p c e -> p (c e)"),
                                    in0=Ss[s3_].t[:, bl].rearrange("p c e -> p (c e)"),
                                    scalar=PT(3, KIND_SAMP, h), in1=pinc[ib].t[:, :], op0=ALU.mult, op1=ALU.add),
                                   reads=[pinc[ib].res, Ss[s3_].res, ptab.res], writes=[Ss[s3_].res])
                            for bl in range(SG):
                                dma(SP, Ss[s3_].dsem, nrs_out[l, b0 + bl, h].rearrange("(c p) e -> p c e", p=128),
                                    Ss[s3_].t[:, bl], reads=[Ss[s3_].res])
                            if pend and gi in (1, 5):
                                S3(pend.pop(0))
                            yield
                        sgrp[0] += len(groups)
                        norm_chain(i, po)
                        pend.append(i)
                        yield
                    while pend:
                        S3(pend.pop(0))
                        yield

                def interleave(ga, gb):
                    gens = [g_ for g_ in (ga, gb) if g_ is not None]
                    while gens:
                        for g_ in list(gens):
                            try:
                                next(g_)
                            except StopIteration:
                                gens.remove(g_)

                interleave(genA(0), None)
                for h in range(NH):
                    interleave(genA(h + 1) if h + 1 < NH else None, genB(h))
                barrier()

            Zm = SB("Zm", [128, NKC, TMAX], BF16, big)
            Zres = [[Res() for _ in tiles] for _ in range(NKC)]
            with ExitStack() as sc:
                zl = [Slot(SB(f"zl{i}", [128, 512], BF16, sc), dsem=dsem(f"zl{i}", sc)) for i in range(2)]
                sact = [Slot(SB(f"sactm{i}", [128, 512], F32, sc)) for i in range(2)]
                mt = [Slot(SB(f"mt{i}", [128, 512], F32, sc)) for i in range(2)]
                pa = [Slot(PS(f"pma{i}", [128, 512], F32, sc)) for i in range(2)]
                pb = [Slot(PS(f"pmb{i}", [128, 512], F32, sc)) for i in range(2)]
                cnt = 0
                npb = WB // 128
                m_iters = [(f, wi) for f in range(NKC) for wi in range(len(wins))]

                def zl_load(idx):
                    f_, wi_ = m_iters[idx]
                    c0_, n_ = wins[wi_]
                    dma(SP, zl[idx % 2].dsem, zl[idx % 2].t[:, 0:n_], zscr[f_][:, c0_:c0_ + n_], reads=[zres[f_][wi_]],
                        writes=[zl[idx % 2].res])
                zl_load(0)
                for k in range(2048 // WB):
                    si_g = WS.get()
                    si_p = WS.get()
                    wg, wp = ring[si_g], ring[si_p]
                    for fl in range(npb):
                        f = k * npb + fl
                        for (c0, n) in wins:
                            b = cnt % 2
                            if cnt + 1 < len(m_iters):
                                zl_load(cnt + 1)
                            cnt += 1
                            til = tiles_in(c0, n)
                            wi = wins.index((c0, n))
                            mm_chain(pa[b].t[:, 0:n], lambda kc: wg.t[:, kc, fl * 128:(fl + 1) * 128],
                                     lambda kc: H[:, kc, c0:c0 + n], [wg.res] + [Hres[i] for i in til], pa[b].res)
                            op(ACT, lambda: nc.scalar.activation(out=sact[b].t[:, 0:n], in_=pa[b].t[:, 0:n], func=AF.Sigmoid),
                               reads=[pa[b].res], writes=[sact[b].res])
                            mm_chain(pb[b].t[:, 0:n], lambda kc: wp.t[:, kc, fl * 128:(fl + 1) * 128],
                                     lambda kc: Y[:, kc, c0:c0 + n],
                                     [wp.res] + [Yres[kc][i] for kc in range(NKC) for i in til], pb[b].res)
                            op(DVE, lambda: nc.vector.tensor_tensor(out=mt[b].t[:, 0:n], in0=pb[b].t[:, 0:n],
                                                                    in1=sact[b].t[:, 0:n], op=ALU.mult),
                               reads=[pb[b].res, sact[b].res], writes=[mt[b].res])
                            op(DVE, lambda: nc.vector.tensor_tensor(out=Zm[:, f, c0:c0 + n], in0=mt[b].t[:, 0:n],
                                                                    in1=zl[b].t[:, 0:n], op=ALU.add),
                               reads=[mt[b].res, zl[b].res], writes=[Zres[f][i] for i in til])
                    WS.release(si_g)
                    WS.release(si_p)
                barrier()

            with ExitStack() as sc:
                xq = [Slot(SB(f"xq{i}", [128, WB], F32, sc), dsem=dsem(f"xq{i}", sc)) for i in range(3)]
                xo = [Slot(SB(f"xo{i}", [128, WB], F32, sc), dsem=dsem(f"xo{i}", sc)) for i in range(3)]
                po_ = [Slot(PS(f"poo{i}", [128, 512], F32, sc)) for i in range(2)]
                dres = [Res() for _ in tiles]
                cnt = 0
                o_iters = [(k, i) for k in range(2048 // WB) for i in range(ntl)]

                def xq_load(idx):
                    k_, i_ = o_iters[idx]
                    dma(SP, xq[idx % 3].dsem, xq[idx % 3].t[:], src_x[tiles[i_].gidx][:, k_ * WB:(k_ + 1) * WB],
                        writes=[xq[idx % 3].res])
                xq_load(0)
                xq_load(1)
                for k in range(2048 // WB):
                    ccol = slice(k * WB, (k + 1) * WB)
                    si = WS.get()
                    wo = ring[si]
                    for i, t in enumerate(tiles):
                        b = cnt % 2
                        b3 = cnt % 3
                        if cnt + 2 < len(o_iters):
                            xq_load(cnt + 2)
                        cnt += 1
                        mm_chain(po_[b].t[0:t.n, 0:WB], lambda kc: Zm[:, kc, t.col:t.col + t.n], lambda kc: wo.t[:, kc, :],
                                 [wo.res] + [Zres[kc][i] for kc in range(NKC)], po_[b].res)
                        if t.n < 128:
                            op(DVE, lambda: nc.vector.memset(xo[b3].t[:], 0.0), writes=[xo[b3].res])
                        op(DVE, lambda: nc.vector.tensor_tensor(out=xo[b3].t[0:t.n, :], in0=po_[b].t[0:t.n, 0:WB],
                                                                in1=xq[b3].t[0:t.n, :], op=ALU.add),
                           reads=[po_[b].res, xq[b3].res], writes=[xo[b3].res])
                        dma(SP, xo[b3].dsem, dst_x[t.gidx][:, ccol], xo[b3].t[:], reads=[xo[b3].res], writes=[dres[i]])
                    WS.release(si)
                SP.wait([(s.dsem, s.dsem.cnt) for s in xo])
                if l == L - 1:
                    xt = [Slot(SB(f"fx{i}", [128, D], F32, sc), dsem=dsem(f"fx{i}", sc)) for i in range(2)]
                    yo = [Slot(SB(f"fy{i}", [128, D], F32, sc), dsem=dsem(f"fy{i}", sc)) for i in range(2)]
                    gb = Slot(SB("fgb", [128, D], F32, sc), dsem=dsem("fgb", sc))
                    junk = Slot(SB("fjunk", [128, D], BF16, sc))
                    ss = [Slot(SB(f"fss{i}", [128, 1], F32, sc)) for i in range(2)]
                    rs = [Slot(SB(f"frs{i}", [128, 1], F32, sc)) for i in range(2)]
                    dma(SP, gb.dsem, gb.t[:], norm_g[L:L + 1, :].partition_broadcast(128), writes=[gb.res])
                    for i, t in enumerate(tiles):
                        b = i % 2
                        dma(SP, xt[b].dsem, xt[b].t[:], dst_x[t.gidx], reads=[dres[i]], writes=[xt[b].res])
                        op(ACT, lambda: nc.scalar.activation(out=junk.t[:], in_=xt[b].t[:], func=AF.Square,
                                                             accum_out=ss[b].t[:]),
                           reads=[xt[b].res], writes=[junk.res, ss[b].res])
                        op(ACT, lambda: nc.scalar.activation(out=ss[b].t[:], in_=ss[b].t[:], func=AF.Sqrt, bias=EPS, scale=1.0 / D),
                           reads=[ss[b].res], writes=[ss[b].res])
                        op(DVE, lambda: nc.vector.reciprocal(out=rs[b].t[:], in_=ss[b].t[:]),
                           reads=[ss[b].res], writes=[rs[b].res])
                        op(DVE, lambda: nc.vector.scalar_tensor_tensor(out=yo[b].t[:], in0=xt[b].t[:], scalar=rs[b].t[:, 0:1],
                                                                       in1=gb.t[:], op0=ALU.mult, op1=ALU.mult),
                           reads=[xt[b].res, rs[b].res, gb.res], writes=[yo[b].res])
                        dma(SP, yo[b].dsem, y_out[t.gidx], yo[b].t[:], reads=[yo[b].res])
                barrier()
            big.close()


_CONSTS = None


def kernel(x_prompt, x_sample, state_pool, state_ret, meta_tokens, norm_gain, w_in, pool_w, pool_scale,
           ret_gn_gain, proj_pool, proj_ret, w_out, final_norm):
    global _CONSTS
    f32 = np.float32
    x_prompt = np.asarray(x_prompt, f32)
    x_sample = np.asarray(x_sample, f32)
    state_pool = np.asarray(state_pool, f32)
    state_ret = np.asarray(state_ret, f32)
    if _CONSTS is None:
        _CONSTS = [host_consts(0), host_consts(1)]
    nc = build_program()

    shared = {
        "w_in": np.ascontiguousarray(np.asarray(w_in, f32)),
        "pool_w": np.ascontiguousarray(np.asarray(pool_w, f32).reshape(L, 4 * 512, 512)),
        "proj_pool": np.ascontiguousarray(np.asarray(proj_pool, f32)),
        "proj_ret": np.ascontiguousarray(np.asarray(proj_ret, f32)),
        "w_out": np.ascontiguousarray(np.asarray(w_out, f32)),
        "norm_g": np.ascontiguousarray(np.concatenate([np.asarray(norm_gain, f32), np.asarray(final_norm, f32)[None]], 0)),
        "gn_gain": np.ascontiguousarray(np.asarray(ret_gn_gain, f32)),
        "pscaleT": np.ascontiguousarray(np.asarray(pool_scale, f32).reshape(L, NKC, 128).transpose(0, 2, 1)),
    }
    in_maps = []
    meta_tile = np.zeros((128, D), f32)
    meta_tile[:N_META] = np.asarray(meta_tokens, f32)
    for c in range(NCORES):
        b, role = c // 2, c % 2
        xin = np.zeros((NT_TOTAL, 128, D), f32)
        if role == 0:
            xin[0] = meta_tile
        xin[1:9] = x_prompt[b, 1024 * role:1024 * (role + 1)].reshape(8, 128, D)
        xin[9] = x_sample[NB_S * c:NB_S * (c + 1)].reshape(128, D)
        m = dict(shared)
        m.update(_CONSTS[role])
        m["xin"] = xin
        m["spool"] = np.ascontiguousarray(state_pool[:, NB_S * c:NB_S * (c + 1)])
        m["sret"] = np.ascontiguousarray(state_ret[:, NB_S * c:NB_S * (c + 1)])
        in_maps.append(m)
    res = run_bass_kernel_spmd(nc, in_maps, core_ids=list(range(NCORES)))
    R = res.results
    y_prompt = np.empty((4, SEQ, D), f32)
    y_sample = np.empty((128, DEC_SEQ, D), f32)
    npp = np.empty((L, 4, 15, D), f32)
    nrp = np.empty((L, 4, NH, DK, DK), f32)
    nps = np.empty((L, 128, 15, D), f32)
    nrs = np.empty((L, 128, NH, DK, DK), f32)
    for c in range(NCORES):
        b, role = c // 2, c % 2
        r = R[c]
        y = np.asarray(r["y"])
        y_sample[NB_S * c:NB_S * (c + 1)] = y[9].reshape(NB_S, DEC_SEQ, D)
        nps[:, NB_S * c:NB_S * (c + 1)] = np.asarray(r["nps"])
        nrs[:, NB_S * c:NB_S * (c + 1)] = np.asarray(r["nrs"])
        y_prompt[b, 1024 * role:1024 * (role + 1)] = y[1:9].reshape(1024, D)
        if role == 1:
            npp[:, b] = np.asarray(r["npp"])
            nrp[:, b] = np.asarray(r["nrp"])
    return (y_prompt, y_sample, npp, nrp, nps, nrs)
```
